# Optimizing a Trainium2 kernel written in Bass

```python
import jax, jax.numpy as jnp
from jax import lax
import numpy as np

D_MODEL = 1024
BATCH = 32
SEQ = 256
DEPTH = 4
DEC_BATCH = 4
DEC_SEQ = 2048
PAST_LEN = 256

GRID_W = 64
HEAD_DIM = 64
GLOB_HEADS = 6
GLOB_KV = 2
RET_HEADS = 4
WIN_HEADS = 6
WIN_KV = 2
WINDOW = 128
BLOCK = 128
RET_CHUNK = 128
ROPE_BASE = 10000.0
QK_SCALE = HEAD_DIM ** -0.5
D_FF = -(-8 * D_MODEL // (3 * 256)) * 256
MIX_WIDTH = (GLOB_HEADS + RET_HEADS + WIN_HEADS) * HEAD_DIM
SPLIT_SIZES = (GLOB_HEADS * HEAD_DIM, GLOB_KV * HEAD_DIM, GLOB_KV * HEAD_DIM,
               RET_HEADS * HEAD_DIM, RET_HEADS * HEAD_DIM, RET_HEADS * HEAD_DIM, RET_HEADS * HEAD_DIM,
               WIN_HEADS * HEAD_DIM, WIN_KV * HEAD_DIM, WIN_KV * HEAD_DIM)
IN_WIDTH = sum(SPLIT_SIZES)
EPS = 1e-6

kernel_name = 'hymba_style_diffusion_retention_swa_step'

F32 = jnp.float32


def rms_norm(x, g):
    xf = x.astype(F32)
    y = xf * lax.rsqrt(jnp.mean(xf * xf, axis=-1, keepdims=True) + EPS)
    return (y * g.astype(F32)).astype(x.dtype)


def head_rms(x):
    xf = x.astype(F32)
    return xf * lax.rsqrt(jnp.mean(xf * xf, axis=-1, keepdims=True) + EPS)


def grid_positions(n):
    rows = n // GRID_W
    row = jnp.broadcast_to(jnp.arange(rows, dtype=F32)[:, None], (rows, GRID_W)).reshape(-1)
    col = jnp.broadcast_to(jnp.arange(GRID_W, dtype=F32)[None, :], (rows, GRID_W)).reshape(-1)
    return row, col


def rope_1d(x, pos):
    half = x.shape[-1] // 2
    freqs = ROPE_BASE ** (-jnp.arange(half, dtype=F32) / half)
    ang = pos[:, None] * freqs[None, :]
    cos = jnp.cos(ang)[None, :, None, :]
    sin = jnp.sin(ang)[None, :, None, :]
    xf = x.astype(F32)
    x1, x2 = xf[..., :half], xf[..., half:]
    return jnp.concatenate([x1 * cos - x2 * sin, x2 * cos + x1 * sin], axis=-1).astype(x.dtype)


def rope_2d(x, row, col):
    h = x.shape[-1] // 2
    return jnp.concatenate([rope_1d(x[..., :h], row), rope_1d(x[..., h:], col)], axis=-1)


def attn_probs(s, sink):
    s = s.astype(F32)
    if sink is None:
        return jax.nn.softmax(s, axis=-1)
    sk = jnp.broadcast_to(sink.astype(F32)[:, :, None, None], s.shape[:-1] + (1,))
    return jax.nn.softmax(jnp.concatenate([s, sk], axis=-1), axis=-1)[..., :-1]


def blocked_attention(q, k, v, sink=None):
    b, nq, h, d = q.shape
    hkv = k.shape[2]
    qb = q.reshape(b, nq // BLOCK, BLOCK, hkv, h // hkv, d).swapaxes(0, 1)

    def one_block(qblk):
        s = jnp.einsum('bqhgd,bkhd->bhgqk', qblk, k)
        p = attn_probs(s, sink).astype(v.dtype)
        return jnp.einsum('bhgqk,bkhd->bqhgd', p, v)

    out = lax.map(one_block, qb)
    return out.swapaxes(0, 1).reshape(b, nq, h, d)


def banded_window_attention(q, k, v, k_ctx, v_ctx, sink):
    b, n, h, d = q.shape
    hkv = k.shape[2]
    nb = n // BLOCK
    qb = q.reshape(b, nb, BLOCK, hkv, h // hkv, d)

    def neighbours(t):
        tb = t.reshape(b, nb, BLOCK, hkv, d)
        tp = jnp.pad(tb, ((0, 0), (1, 1), (0, 0), (0, 0), (0, 0)))
        return jnp.concatenate([tp[:, :-2], tp[:, 1:-1], tp[:, 2:]], axis=2)

    kn, vn = neighbours(k), neighbours(v)
    qi = jnp.arange(BLOCK)[:, None]
    kj = jnp.arange(3 * BLOCK)[None, :] - BLOCK
    within = jnp.abs(kj - qi) <= WINDOW
    key_pos = jnp.arange(nb)[:, None] * BLOCK + kj
    valid = (key_pos >= 0) & (key_pos < n)
    mask = within[None] & valid[:, None, :]
    s_loc = jnp.einsum('bnqhgd,bnkhd->bnhgqk', qb, kn).astype(F32)
    s_loc = jnp.where(mask[None, :, None, None], s_loc, -jnp.inf)
    s_ctx = jnp.einsum('bnqhgd,bkhd->bnhgqk', qb, k_ctx).astype(F32)
    p = attn_probs(jnp.concatenate([s_loc, s_ctx], axis=-1), sink).astype(v.dtype)
    nloc = 3 * BLOCK
    out = (jnp.einsum('bnhgqk,bnkhd->bnqhgd', p[..., :nloc], vn)
           + jnp.einsum('bnhgqk,bkhd->bnqhgd', p[..., nloc:], v_ctx))
    return out.reshape(b, n, h, d)


def retention_direction(q, k, v, log_gamma, s0):
    b, n, h, d = q.shape
    nc = n // RET_CHUNK

    def to_chunks(t):
        return t.reshape(b, nc, RET_CHUNK, h, d).swapaxes(0, 1)

    idx = jnp.arange(RET_CHUNK, dtype=F32)
    diff = idx[:, None] - idx[None, :]
    inner_decay = jnp.where(diff >= 0, jnp.exp(jnp.maximum(diff, 0.0)[None] * log_gamma[:, None, None]), 0.0)
    q_decay = jnp.exp((idx + 1.0)[:, None] * log_gamma[None, :])
    k_decay = jnp.exp((RET_CHUNK - 1.0 - idx)[:, None] * log_gamma[None, :])
    chunk_decay = jnp.exp(RET_CHUNK * log_gamma)

    def step(s, qkv):
        qc, kc, vc = qkv
        a = jnp.einsum('bnhd,bmhd->bhnm', qc, kc) * inner_decay
        o = (jnp.einsum('bhnm,bmhe->bnhe', a, vc)
             + jnp.einsum('bnhd,bhde->bnhe', qc, s) * q_decay[None, :, :, None])
        s = (s * chunk_decay[None, :, None, None]
             + jnp.einsum('bmhd,bmhe->bhde', kc * k_decay[None, :, :, None], vc))
        return s, o

    s_fin, out = lax.scan(step, s0, (to_chunks(q), to_chunks(k), to_chunks(v)))
    return out.swapaxes(0, 1).reshape(b, n, h, d), s_fin


def bidir_retention(q, k, v, lg_f, lg_b, s_f0, s_b0):
    o_f, s_f = retention_direction(q, k, v, lg_f, s_f0)
    o_b, s_b = retention_direction(q[:, ::-1], k[:, ::-1], v[:, ::-1], lg_b, s_b0)
    return o_f + o_b[:, ::-1], s_f, s_b


def modulation(cond, w_mod, b_mod):
    m = jnp.einsum('...d,de->...e', jax.nn.silu(cond), w_mod) + b_mod
    return jnp.split(jnp.expand_dims(m, -2), 6, axis=-1)


def project(h, w_in):
    p = jnp.einsum('bnd,de->bne', h, w_in)
    b, n, _ = p.shape
    offsets = []
    acc = 0
    for sz in SPLIT_SIZES[:-1]:
        acc += sz
        offsets.append(acc)
    return [t.reshape(b, n, -1, HEAD_DIM) for t in jnp.split(p, offsets, axis=-1)]


def trunk_layer(x, cond, lp, cache=None, pos=None):
    sh1, sc1, gt1, sh2, sc2, gt2 = modulation(cond, lp['w_mod'], lp['b_mod'])
    h = rms_norm(x, lp['g_pre_mix']) * (1 + sc1) + sh1
    gq, gk, gv, rq, rk, rv, rg, wq, wk, wv = project(h, lp['w_in'])
    gq = rms_norm(gq, lp['g_q'])
    gk = rms_norm(gk, lp['g_k'])
    sink = lp['win_sink'].reshape(WIN_KV, WIN_HEADS // WIN_KV)
    lg_f = jax.nn.log_sigmoid(lp['ret_decay_fwd'].astype(F32))
    lg_b = jax.nn.log_sigmoid(lp['ret_decay_bwd'].astype(F32))
    rq32 = rq.astype(F32)
    rk32 = rk.astype(F32) * QK_SCALE
    rv32 = rv.astype(F32)
    b = x.shape[0]
    if cache is None:
        zeros = jnp.zeros((b, RET_HEADS, HEAD_DIM, HEAD_DIM), F32)
        glob = blocked_attention(gq * QK_SCALE, gk, gv)
        ret, s_f, s_b = bidir_retention(rq32, rk32, rv32, lg_f, lg_b, zeros, zeros)
        win = blocked_attention(wq * QK_SCALE, wk, wv, sink)
        new = (gk, gv, s_f.astype(x.dtype), s_b.astype(x.dtype), wk, wv)
    else:
        ck, cv, sf0, sb0, cwk, cwv = cache
        row, col = pos
        k_all = jnp.concatenate([rope_2d(gk, row, col), ck], axis=1)
        v_all = jnp.concatenate([gv, cv], axis=1)
        glob = blocked_attention(rope_2d(gq, row, col) * QK_SCALE, k_all, v_all)
        ret, _, _ = bidir_retention(rq32, rk32, rv32, lg_f, lg_b, sf0.astype(F32), sb0.astype(F32))
        win = banded_window_attention(rope_2d(wq, row, col) * QK_SCALE, rope_2d(wk, row, col), wv,
                                      cwk, cwv, sink)
        new = None
    ret = (head_rms(ret) * jax.nn.silu(rg.astype(F32))).astype(x.dtype)
    n = x.shape[1]
    merged = jnp.concatenate([glob.reshape(b, n, -1), ret.reshape(b, n, -1), win.reshape(b, n, -1)], axis=-1)
    mixed = jnp.einsum('bne,ed->bnd', merged, lp['w_out'])
    x = x + gt1 * rms_norm(mixed, lp['g_post_mix'])
    h = rms_norm(x, lp['g_pre_ffn']) * (1 + sc2) + sh2
    a, g = jnp.split(jnp.einsum('bnd,df->bnf', h, lp['w_gate_up']), 2, axis=-1)
    ff = jnp.einsum('bnf,fd->bnd', jax.nn.silu(a) * g, lp['w_down'])
    x = x + gt2 * rms_norm(ff, lp['g_post_ffn'])
    return x, new


def setup_inputs(seed: int = 0) -> dict:
    key = jax.random.key(seed)
    ks = jax.random.split(key, 32)
    nrm = jax.random.normal
    ret_init = jnp.log(2.0 ** (5.0 + jnp.arange(RET_HEADS, dtype=F32)) - 1.0)
    return {
        'x_prompt': nrm(ks[0], (BATCH, SEQ, D_MODEL), F32),
        'x_sample': nrm(ks[1], (DEC_BATCH, DEC_SEQ, D_MODEL), F32),
        'cache_glob_k': nrm(ks[2], (DEC_BATCH, DEPTH, PAST_LEN, GLOB_KV, HEAD_DIM), F32),
        'cache_glob_v': nrm(ks[3], (DEC_BATCH, DEPTH, PAST_LEN, GLOB_KV, HEAD_DIM), F32),
        'state_ret_fwd': 0.5 * nrm(ks[4], (DEC_BATCH, DEPTH, RET_HEADS, HEAD_DIM, HEAD_DIM), F32),
        'state_ret_bwd': 0.5 * nrm(ks[5], (DEC_BATCH, DEPTH, RET_HEADS, HEAD_DIM, HEAD_DIM), F32),
        'cache_win_k': nrm(ks[6], (DEC_BATCH, DEPTH, PAST_LEN, WIN_KV, HEAD_DIM), F32),
        'cache_win_v': nrm(ks[7], (DEC_BATCH, DEPTH, PAST_LEN, WIN_KV, HEAD_DIM), F32),
        'c': nrm(ks[8], (DEC_BATCH, D_MODEL), F32),
        'c_ctx': nrm(ks[9], (D_MODEL,), F32),
        'w_mod': 0.5 * D_MODEL ** -0.5 * nrm(ks[10], (DEPTH, D_MODEL, 6 * D_MODEL), F32),
        'b_mod': 0.01 * nrm(ks[11], (DEPTH, 6 * D_MODEL), F32),
        'g_pre_mix': 1.0 + 0.05 * nrm(ks[12], (DEPTH, D_MODEL), F32),
        'g_post_mix': 1.0 + 0.05 * nrm(ks[13], (DEPTH, D_MODEL), F32),
        'g_pre_ffn': 1.0 + 0.05 * nrm(ks[14], (DEPTH, D_MODEL), F32),
        'g_post_ffn': 1.0 + 0.05 * nrm(ks[15], (DEPTH, D_MODEL), F32),
        'w_in': D_MODEL ** -0.5 * nrm(ks[16], (DEPTH, D_MODEL, IN_WIDTH), F32),
        'g_q': 1.0 + 0.05 * nrm(ks[17], (DEPTH, HEAD_DIM), F32),
        'g_k': 1.0 + 0.05 * nrm(ks[18], (DEPTH, HEAD_DIM), F32),
        'ret_decay_fwd': ret_init[None, :] + 0.1 * nrm(ks[19], (DEPTH, RET_HEADS), F32),
        'ret_decay_bwd': ret_init[None, :] + 0.1 * nrm(ks[20], (DEPTH, RET_HEADS), F32),
        'win_sink': 0.5 * nrm(ks[21], (DEPTH, WIN_HEADS), F32),
        'w_out': MIX_WIDTH ** -0.5 * nrm(ks[22], (DEPTH, MIX_WIDTH, D_MODEL), F32),
        'w_gate_up': D_MODEL ** -0.5 * nrm(ks[23], (DEPTH, D_MODEL, 2 * D_FF), F32),
        'w_down': D_FF ** -0.5 * nrm(ks[24], (DEPTH, D_FF, D_MODEL), F32),
    }


def reference(x_prompt, x_sample, cache_glob_k, cache_glob_v, state_ret_fwd, state_ret_bwd,
              cache_win_k, cache_win_v, c, c_ctx, w_mod, b_mod, g_pre_mix, g_post_mix,
              g_pre_ffn, g_post_ffn, w_in, g_q, g_k, ret_decay_fwd, ret_decay_bwd, win_sink,
              w_out, w_gate_up, w_down):
    def layer_params(l):
        return {'w_mod': w_mod[l], 'b_mod': b_mod[l], 'g_pre_mix': g_pre_mix[l],
                'g_post_mix': g_post_mix[l], 'g_pre_ffn': g_pre_ffn[l], 'g_post_ffn': g_post_ffn[l],
                'w_in': w_in[l], 'g_q': g_q[l], 'g_k': g_k[l], 'ret_decay_fwd': ret_decay_fwd[l],
                'ret_decay_bwd': ret_decay_bwd[l], 'win_sink': win_sink[l], 'w_out': w_out[l],
                'w_gate_up': w_gate_up[l], 'w_down': w_down[l]}

    y = x_prompt
    gk_l, gv_l, sf_l, sb_l, wk_l, wv_l = [], [], [], [], [], []
    for l in range(DEPTH):
        y, (gk, gv, sf, sb, wk, wv) = trunk_layer(y, c_ctx, layer_params(l))
        gk_l.append(gk); gv_l.append(gv); sf_l.append(sf)
        sb_l.append(sb); wk_l.append(wk); wv_l.append(wv)
    new_glob_k = jnp.stack(gk_l, axis=1)
    new_glob_v = jnp.stack(gv_l, axis=1)
    new_ret_fwd = jnp.stack(sf_l, axis=1)
    new_ret_bwd = jnp.stack(sb_l, axis=1)
    new_win_k = jnp.stack(wk_l, axis=1)
    new_win_v = jnp.stack(wv_l, axis=1)

    pos = grid_positions(x_sample.shape[1])
    z = x_sample
    for l in range(DEPTH):
        cache = (cache_glob_k[:, l], cache_glob_v[:, l], state_ret_fwd[:, l], state_ret_bwd[:, l],
                 cache_win_k[:, l], cache_win_v[:, l])
        z, _ = trunk_layer(z, c, layer_params(l), cache=cache, pos=pos)

    return (y, z, new_glob_k, new_glob_v, new_ret_fwd, new_ret_bwd, new_win_k, new_win_v)
```

```python
import numpy as np
import concourse.bass as bass
import concourse.mybir as mybir
from concourse.bass_utils import run_bass_kernel_spmd

F32 = mybir.dt.float32
BF16 = mybir.dt.bfloat16
AF = mybir.ActivationFunctionType
ALU = mybir.AluOpType

L = 4
NT = 2048
GT = 512
NG = 4
DFF = 2816
GQ, GK, GV, RQ, RK, RV, RG, WQ, WK, WV = 0, 384, 512, 640, 896, 1152, 1408, 1664, 2048, 2176
SM = 96
NEG = -30000.0
T_GB, T_WB, T_KF, T_KB, T_EPS, T_AC, T_BC, T_ONE, T_NA, T_NB = 0, 144, 146, 162, 178, 179, 180, 181, 182, 183
NTAB = 184


class Buf:
    def __init__(self, name, excl=False):
        self.name = name
        self.w = None
        self.r = {}
        self.excl = excl


class Sched:
    def __init__(self, nc):
        self.nc = nc
        self.eng = {'pe': nc.tensor, 'act': nc.scalar, 'dve': nc.vector, 'pool': nc.gpsimd, 'sp': nc.sync}
        self.sem = {k: nc.alloc_semaphore(name=f"s_{k}") for k in self.eng}
        self.cnt = {k: 0 for k in self.eng}
        self.waited = {k: {} for k in self.eng}
        self.semobj = dict(self.sem)
        self.dma_sems = {}

    def _wait(self, e, key, val):
        if self.waited[e].get(key, 0) >= val:
            return
        self.eng[e].wait_ge(self.semobj[key], val)
        self.waited[e][key] = val

    def _deps(self, e, reads, writes):
        for b in reads:
            if b.w is not None:
                self._wait(e, *b.w)
            if b.excl:
                for en, (k, v) in b.r.items():
                    if k != e:
                        self._wait(e, k, v)
        for b in writes:
            if b.w is not None and (b.w[0] != e or e != 'pe'):
                self._wait(e, *b.w)
            for en, (k, v) in b.r.items():
                if k != e or e != 'pe':
                    self._wait(e, k, v)

    def op(self, e, fn, reads=(), writes=()):
        self._deps(e, reads, writes)
        inst = fn(self.eng[e])
        self.cnt[e] += 1
        inst.then_inc(self.sem[e], 1)
        c = self.cnt[e]
        for b in writes:
            b.w = (e, c); b.r = {}
        for b in reads:
            b.r[e] = (e, c)
        return inst

    def mm(self, out, lhsT, rhs, start, stop, reads, writes):
        e = 'pe'
        self._deps(e, reads, writes if start else ())
        inst = self.eng[e].matmul(out, lhsT=lhsT, rhs=rhs, start=start, stop=stop)
        self.cnt[e] += 1
        inst.then_inc(self.sem[e], 1)
        c = self.cnt[e]
        for b in reads:
            b.r[e] = (e, c)
        for b in writes:
            b.w = (e, c)
            if start:
                b.r = {}
        return inst

    def dma(self, e, out, in_, reads=(), writes=(), semname='dma0'):
        if semname not in self.dma_sems:
            s = self.nc.alloc_semaphore(name=f"d_{semname}")
            self.dma_sems[semname] = [s, 0]
            self.semobj[semname] = s
        self._deps(e, reads, writes)
        inst = self.eng[e].dma_start(out=out, in_=in_)
        ent = self.dma_sems[semname]
        ent[1] += 16
        inst.then_inc(ent[0], 16)
        for b in writes:
            b.w = (semname, ent[1]); b.r = {}
        for b in reads:
            b.r['dma_' + semname] = (semname, ent[1])
        return inst

    def barrier(self):
        for e in self.eng:
            for o in self.eng:
                if o != e and self.cnt[o] > 0:
                    self._wait(e, o, self.cnt[o])
            for s, ent in self.dma_sems.items():
                if ent[1] > 0:
                    self._wait(e, s, ent[1])


class _Stop(Exception):
    pass


def build(stop=None):
    nc = bass.Bass("TRN2", target_bir_lowering=False)
    S = Sched(nc)

    def din(name, shape, dt=F32):
        return nc.dram_tensor(name, list(shape), dt, kind="ExternalInput").ap()

    def dout(name, shape):
        return nc.dram_tensor(name, list(shape), F32, kind="ExternalOutput").ap()

    xT_d = din("xT", [8, 128, NT])
    w_in_d = din("w_in", [L, 1024, 2304])
    w_out_d = din("w_out", [L, 1024, 1024])
    w_gu_d = din("w_gate_up", [L, 1024, 2 * DFF])
    w_dn_d = din("w_down", [L, DFF, 1024])
    w_mod_d = din("w_mod", [L, 1024, 6144])
    smalls_d = din("smalls", [128, L * SM])
    cond_d = din("condT", [128, 8])
    rope_d = din("rope", [2, 128, NT])
    tabs_d = din("tabs", [128, NTAB])
    cbf_d = din("cbf", [128, 1024])
    rt_d = din("rt", [128, 640])
    ckT_d = din("ckT", [L, 2, 128, 256])
    cv_d = din("cv", [L, 256, 256])
    st0_d = din("st0", [L, 2, 64, 256])
    yT_d = dout("yT", [8, 128, NT])
    okT_d = dout("okT", [L, 2, 128, NT])
    ov_d = dout("ov", [L, NT, 256])
    ost_d = dout("ost", [L, 16, 128, 256])
    mscr_d = nc.dram_tensor("mscr", [NG, 128, 8, GT], BF16, kind="Internal").ap()

    def sb(name, shape, dt=F32):
        return nc.alloc_sbuf_tensor(name, list(shape), dt)

    xT = sb("xT_sb", [128, 8, NT]); X = [[Buf(f"x{g}_{j}") for j in range(8)] for g in range(NG)]
    rstdA = sb("rstdA", [128, NT]); RA = [Buf(f"ra{g}") for g in range(NG)]
    tabs = sb("tabs_sb", [128, NTAB]); TABS = Buf("tabs")
    smalls = sb("smalls_sb", [128, L * SM]); SMALLS = Buf("smalls")
    cbf = sb("cbf_sb", [128, 1024], BF16); CBF = Buf("cbf")
    rt = sb("rt_sb", [128, 640]); RT = Buf("rt")
    cond = sb("cond_sb", [128, 8]); COND = Buf("cond")
    scond = sb("scond", [128, 8], BF16); SCOND = Buf("scond")
    modt = [sb(f"mod{l}", [128, 48]) for l in range(L)]; MOD = [Buf(f"mod{l}") for l in range(L)]
    coef = [sb(f"coef{l}", [128, 32]) for l in range(L)]; COEF = [Buf(f"coef{l}") for l in range(L)]
    ones_bf = cbf[:, 0:128]; onesg_bf = cbf[:, 128:256]; rperm_bf = cbf[:, 384:512]
    mprev = [cbf[:, 512:640], cbf[:, 640:768]]; mnext = [cbf[:, 768:896], cbf[:, 896:1024]]
    NSLOT = 3
    wslot = [sb(f"wslot{i}", [128, 4096], BF16) for i in range(NSLOT)]; WS = [Buf(f"ws{i}") for i in range(NSLOT)]
    wctr = [0]
    NTF = 6
    tf = [sb(f"tf{i}", [128, GT]) for i in range(NTF)]; TF = [Buf(f"tf{i}") for i in range(NTF)]
    tfc = [0]
    NTB = 4
    tb = [sb(f"tb{i}", [128, GT], BF16) for i in range(NTB)]; TB = [Buf(f"tb{i}") for i in range(NTB)]
    tbc = [0]
    hg = [sb(f"hg{i}", [128, 8, GT], BF16) for i in range(2)]; HG = [Buf(f"hg{i}") for i in range(2)]
    psn = ["A", "M", "S0", "S1", "O0", "O1", "R0", "R1"]
    ps = {n: nc.alloc_psum_tensor(f"ps{n}", [128, GT], F32) for n in psn}
    PS = {n: Buf(f"ps{n}", excl=True) for n in psn}

    def ntf():
        i = tfc[0] % NTF; tfc[0] += 1
        return tf[i], TF[i]

    def ntb():
        i = tbc[0] % NTB; tbc[0] += 1
        return tb[i], TB[i]

    def tcol(c):
        return tabs[:, c:c + 1]

    def wload(pieces, slot=None):
        if slot is None:
            i = wctr[0] % NSLOT; wctr[0] += 1
        else:
            i = slot
        for dv, src in pieces:
            S.dma('pool', dv(wslot[i]), src, writes=[WS[i]], semname=f"w{i}")
        return wslot[i], WS[i]

    def wview(slot, ncols):
        return slot[:, 0:8 * ncols].rearrange("p (k e) -> p k e", k=8)

    def wsrc(wd, l, c0, n):
        return wd[l].rearrange("(k p) e -> p k e", p=128)[:, :, c0:c0 + n]

    S.dma('sp', tabs[:], tabs_d, writes=[TABS], semname='c0tabs')
    S.dma('sp', smalls[:], smalls_d, writes=[SMALLS], semname='c0b')
    S.dma('sp', rt[:], rt_d, writes=[RT], semname='c0c')
    S.dma('sp', cond[:], cond_d, writes=[COND], semname='c0d')
    S.dma('pool', cbf[:], cbf_d, writes=[CBF], semname='c1')
    for g in range(NG):
        for j in range(8):
            S.dma('sp', xT[:, j, g * GT:(g + 1) * GT], xT_d[j, :, g * GT:(g + 1) * GT], writes=[X[g][j]], semname=f"x{g}")
        for j in range(8):
            X[g][j].w = (f"x{g}", S.dma_sems[f"x{g}"][1])
    t0, T0 = ntf()
    S.op('act', lambda e: e.activation(out=t0[:, 0:8], in_=cond[:], func=AF.Exp, scale=-1.0), reads=[COND], writes=[T0])
    S.op('dve', lambda e: e.tensor_scalar(out=t0[:, 0:8], in0=t0[:, 0:8], scalar1=1.0, scalar2=None, op0=ALU.add), reads=[T0], writes=[T0])
    S.op('dve', lambda e: e.reciprocal(out=t0[:, 0:8], in_=t0[:, 0:8]), reads=[T0], writes=[T0])
    S.op('dve', lambda e: e.tensor_tensor(out=scond[:], in0=t0[:, 0:8], in1=cond[:], op=ALU.mult), reads=[T0, COND], writes=[SCOND])

    def sm(l, off, n):
        return smalls[:, l * SM + off: l * SM + off + n]

    MW = {}

    def mod_load(l, t, slot=None):
        MW[(l, t)] = wload([(lambda s_: wview(s_, 512), wsrc(w_mod_d, l, t * 512, 512))], slot=slot)

    def mod_mm(l, t):
        wt, WT = MW.pop((l, t))
        wv = wview(wt, 512)
        pm = ps["M"]
        for ec in range(4):
            for kc in range(8):
                S.mm(pm[:, ec:ec + 1], wv[:, kc, ec * 128:(ec + 1) * 128], scond[:, kc:kc + 1], kc == 0, kc == 7,
                     reads=[WT, SCOND], writes=[PS["M"]])
        S.op('dve', lambda e: e.tensor_tensor(out=modt[l][:, t * 4:(t + 1) * 4], in0=pm[:, 0:4], in1=sm(l, t * 4, 4), op=ALU.add),
             reads=[PS["M"], SMALLS], writes=[MOD[l]])

    def emit_mod(l):
        for t in range(12):
            mod_load(l, t)
            mod_mm(l, t)
        mod_finish(l)

    def mod_finish(l):
        S.op('dve', lambda e: e.scalar_tensor_tensor(out=coef[l][:, 0:8], in0=modt[l][:, 8:16], scalar=1.0, op0=ALU.add, in1=sm(l, 48, 8), op1=ALU.mult),
             reads=[MOD[l], SMALLS], writes=[COEF[l]])
        S.op('dve', lambda e: e.tensor_tensor(out=coef[l][:, 8:16], in0=modt[l][:, 16:24], in1=sm(l, 56, 8), op=ALU.mult),
             reads=[MOD[l], SMALLS], writes=[COEF[l]])
        S.op('dve', lambda e: e.scalar_tensor_tensor(out=coef[l][:, 16:24], in0=modt[l][:, 32:40], scalar=1.0, op0=ALU.add, in1=sm(l, 64, 8), op1=ALU.mult),
             reads=[MOD[l], SMALLS], writes=[COEF[l]])
        S.op('dve', lambda e: e.tensor_tensor(out=coef[l][:, 24:32], in0=modt[l][:, 40:48], in1=sm(l, 72, 8), op=ALU.mult),
             reads=[MOD[l], SMALLS], writes=[COEF[l]])

    def emit_stats(g, src_fn, SRC, nchunks=8):
        gs = slice(g * GT, (g + 1) * GT)
        for j in range(nchunks):
            q, Q = ntb()
            if j % 2 == 0:
                S.op('act', lambda e: e.activation(out=q[:], in_=src_fn(j), func=AF.Square), reads=[SRC[j]], writes=[Q])
            else:
                S.op('dve', lambda e: e.tensor_tensor(out=q[:], in0=src_fn(j), in1=src_fn(j), op=ALU.mult), reads=[SRC[j]], writes=[Q])
            S.mm(ps["M"][:], ones_bf, q[:], j == 0, j == nchunks - 1, reads=[Q, CBF], writes=[PS["M"]])
        t, T = ntf()
        S.op('act', lambda e: e.activation(out=t[:], in_=ps["M"][:], func=AF.Ln, scale=1.0 / 1024.0, bias=tcol(T_EPS)),
             reads=[PS["M"], TABS], writes=[T])
        S.op('act', lambda e: e.activation(out=rstdA[:, gs], in_=t[:], func=AF.Exp, scale=-0.5), reads=[T], writes=[RA[g]])

    def emit_h(l, g, hi, a_off, b_off):
        gs = slice(g * GT, (g + 1) * GT)
        for j in range(8):
            t, T = ntf()
            S.op('dve', lambda e: e.scalar_tensor_tensor(out=t[:], in0=xT[:, j, gs], scalar=coef[l][:, a_off + j:a_off + j + 1], op0=ALU.mult,
                                                         in1=rstdA[:, gs], op1=ALU.mult),
                 reads=[X[g][j], COEF[l], RA[g]], writes=[T])
            S.op('act', lambda e: e.activation(out=hg[hi][:, j, :], in_=t[:], func=AF.Identity, bias=modt[l][:, b_off + j:b_off + j + 1]),
                 reads=[T, MOD[l]], writes=[HG[hi]])

    def proj_fm(wv, WT, c, hi, psname="A"):
        for kc in range(8):
            S.mm(ps[psname][:], wv[:, kc, c * 128:(c + 1) * 128], hg[hi][:, kc, :], kc == 0, kc == 7,
                 reads=[WT, HG[hi]], writes=[PS[psname]])

    def normrope_ph(l, g, gcol, ropeT, ROPE, final_fn, pa="A"):
        gs = slice(g * GT, (g + 1) * GT)
        st = {}

        def p1():
            q, Q = ntb()
            S.op('act', lambda e: e.activation(out=q[:], in_=ps[pa][:], func=AF.Square), reads=[PS[pa]], writes=[Q])
            st['q'] = (q, Q)

        def p2():
            q, Q = st['q']
            S.mm(ps["M"][:], onesg_bf, q[:], True, True, reads=[Q, CBF], writes=[PS["M"]])
            t, T = ntf()
            S.op('act', lambda e: e.activation(out=t[:], in_=ps["M"][:], func=AF.Ln, scale=tcol(T_AC), bias=tcol(T_BC)),
                 reads=[PS["M"], TABS], writes=[T])
            S.op('act', lambda e: e.activation(out=t[:], in_=t[:], func=AF.Exp, scale=-0.5), reads=[T], writes=[T])
            kn, KN = ntf()
            S.op('dve', lambda e: e.scalar_tensor_tensor(out=kn[:], in0=ps[pa][:], scalar=gcol, op0=ALU.mult, in1=t[:], op1=ALU.mult),
                 reads=[PS[pa], SMALLS, T], writes=[KN])
            st['kn'] = (kn, KN)

        def p2b():
            kn, KN = st['kn']
            kb_, KB_ = ntb()
            S.op('dve', lambda e: e.tensor_copy(out=kb_[:], in_=kn[:]), reads=[KN], writes=[KB_])
            st['kb'] = (kb_, KB_)

        def p3():
            kn, KN = st['kn']; kb_, KB_ = st['kb']
            S.mm(ps["M"][:], rperm_bf, kb_[:], True, True, reads=[KB_, CBF], writes=[PS["M"]])
            t2, T2 = ntf()
            S.op('dve', lambda e: e.tensor_tensor(out=t2[:], in0=ps["M"][:], in1=ropeT[:, 1, gs], op=ALU.mult), reads=[PS["M"], ROPE], writes=[T2])
            S.op('dve', lambda e: e.tensor_tensor(out=kn[:], in0=kn[:], in1=ropeT[:, 0, gs], op=ALU.mult), reads=[KN, ROPE], writes=[KN])
            S.op('dve', lambda e: e.tensor_tensor(out=kn[:], in0=kn[:], in1=t2[:], op=ALU.add), reads=[KN, T2], writes=[KN])

        def p3b():
            kn, KN = st['kn']
            final_fn(kn, KN)
        return p1, p2, p2b, p3, p3b

    def normrope(l, g, gcol, ropeT, ROPE, final_fn):
        for p in normrope_ph(l, g, gcol, ropeT, ROPE, final_fn):
            p()

    PW = {}

    def load_s1w(l):
        PW['s1'] = wload([
            (lambda s_: wview(s_, 512)[:, :, 0:64], wsrc(w_in_d, l, GK, 64)),
            (lambda s_: wview(s_, 512)[:, :, 64:128], wsrc(w_in_d, l, WK, 64)),
            (lambda s_: wview(s_, 512)[:, :, 128:192], wsrc(w_in_d, l, GK + 64, 64)),
            (lambda s_: wview(s_, 512)[:, :, 192:256], wsrc(w_in_d, l, WK + 64, 64)),
            (lambda s_: wview(s_, 512)[:, :, 256:384], wsrc(w_in_d, l, GV, 128)),
            (lambda s_: wview(s_, 512)[:, :, 384:512], wsrc(w_in_d, l, WV, 128)),
        ], slot=0)

    def load_passA(l):
        PW['pa0'] = wload([(lambda s_, h=h, d=d: wview(s_, 512)[:, :, h * 128 + d * 64:h * 128 + d * 64 + 64], wsrc(w_in_d, l, RK + h * 64, 64))
                           for h in range(4) for d in range(2)], slot=1)
        PW['pa1'] = wload([(lambda s_: wview(s_, 512)[:, :, 0:256], wsrc(w_in_d, l, RV, 256)),
                           (lambda s_: wview(s_, 512)[:, :, 256:512], wsrc(w_in_d, l, RK, 256))], slot=2)

    def wo_view(slot):
        return slot[:, 0:4096].rearrange("p (c e) -> p c e", c=4)

    def load_wo_half(l, half, slot):
        pieces = []
        for c in range(half * 4, half * 4 + 4):
            cc = c % 4
            if c < 6:
                pieces.append((lambda s_, cc=cc: wo_view(s_)[0:64, cc, :], w_out_d[l, c * 64:(c + 1) * 64, :]))
                pieces.append((lambda s_, cc=cc: wo_view(s_)[64:128, cc, :], w_out_d[l, 640 + c * 64:640 + (c + 1) * 64, :]))
            elif c == 6:
                pieces.append((lambda s_, cc=cc: wo_view(s_)[:, cc, :], w_out_d[l, 384:512, :]))
            else:
                pieces.append((lambda s_, cc=cc: wo_view(s_)[:, cc, :], w_out_d[l, 512:640, :]))
        PW['wo%d' % half] = wload(pieces, slot=slot)

    def ck(name):
        if stop == name:
            raise _Stop()

    try:
        ck('prologue')
        emit_mod(0)
        load_s1w(0)
        ck('mod0')
        for l in range(L):
            with nc.sbuf_tensor(f"kTg{l}", [128, 2, 2304], BF16) as kTg, nc.sbuf_tensor(f"kTw{l}", [128, 2, 2304], BF16) as kTw, \
                    nc.sbuf_tensor(f"vall{l}", [128, 18, 256], BF16) as vall, nc.sbuf_tensor(f"ropeT{l}", [128, 2, NT], F32) as ropeT, \
                    nc.sbuf_tensor(f"qTg{l}", [128, 2, 6, GT], BF16) as qTg, nc.sbuf_tensor(f"mg{l}", [128, 6, GT], BF16) as mg, \
                    nc.sbuf_tensor(f"pT{l}", [128, 4, GT], BF16) as pT, nc.sbuf_tensor(f"vout{l}", [128, 2, 256], F32) as vout, \
                    nc.sbuf_tensor(f"esb{l}", [128, 8], F32) as esb, \
                    nc.sbuf_tensor(f"pTc{l}", [128, 2, GT], BF16) as pTc:
                KT = Buf("kT"); VALL = Buf("vall"); ROPE = Buf("rope"); QT = [Buf("qt0"), Buf("qt1")]; MG = Buf("mg")
                PT = [Buf(f"pt{i}") for i in range(4)]; VOUT = [Buf("vo0"), Buf("vo1")]; ESB = Buf("esb"); PTC = [Buf("ptc0"), Buf("ptc1")]
                S.dma('sp', ropeT[:, 0, :], rope_d[0], writes=[ROPE], semname='rope')
                S.dma('sp', ropeT[:, 1, :], rope_d[1], writes=[ROPE], semname='rope')
                S.op('dve', lambda e: e.memset(kTg[64:128, :, :], 0.0), writes=[KT])
                S.op('dve', lambda e: e.memset(kTw[0:64, :, :], 0.0), writes=[KT])
                for kvh in range(2):
                    S.dma('pool', kTg[0:64, kvh, NT:NT + 256], ckT_d[l, kvh, 0:64, :], writes=[KT], semname='ck')
                    S.dma('pool', kTw[64:128, kvh, NT:NT + 256], ckT_d[l, kvh, 64:128, :], writes=[KT], semname='ck')
                S.dma('pool', vall[:, 16:18, :], cv_d[l].rearrange("(b p) c -> p b c", p=128), writes=[VALL], semname='cv')
                S.op('act', lambda e: e.activation(out=esb[:, 0:6], in_=sm(l, 90, 6), func=AF.Exp), reads=[SMALLS], writes=[ESB])
                for g in range(NG):
                    emit_stats(g, lambda j, g=g: xT[:, j, g * GT:(g + 1) * GT], X[g])
                ck(f'stats{l}')
                ck('s1a')
                w0, W0 = wload([(lambda s_, h=h: wview(s_, 512)[:, :, h * 128:h * 128 + 64], wsrc(w_in_d, l, GQ + h * 64, 64)) for h in range(4)] +
                               [(lambda s_, h=h: wview(s_, 512)[:, :, h * 128 + 64:h * 128 + 128], wsrc(w_in_d, l, WQ + h * 64, 64)) for h in range(4)], slot=1)
                w1, W1 = wload([(lambda s_, h=h: wview(s_, 512)[:, :, (h - 4) * 128:(h - 4) * 128 + 64], wsrc(w_in_d, l, GQ + h * 64, 64)) for h in (4, 5)] +
                               [(lambda s_, h=h: wview(s_, 512)[:, :, (h - 4) * 128 + 64:(h - 4) * 128 + 128], wsrc(w_in_d, l, WQ + h * 64, 64)) for h in (4, 5)], slot=2)

                wt, WT = PW['s1']
                for g in range(NG):
                    gs = slice(g * GT, (g + 1) * GT)
                    hi = g % 2
                    emit_h(l, g, hi, 0, 0)
                    ck('s1b')
                    wv = wview(wt, 512)
                    ck('s1c')
                    def vblock(b, bank, g=g, hi=hi):
                        blk = g * 4 + b
                        for kc in range(8):
                            S.mm(ps[bank][:, 0:256], hg[hi][:, kc, b * 128:(b + 1) * 128], wv[:, kc, 256:512], kc == 0, kc == 7,
                                 reads=[WT, HG[hi]], writes=[PS[bank]])
                        vi = blk % 2
                        S.op('act', lambda e: e.activation(out=vall[:, blk, :], in_=ps[bank][:, 0:256], func=AF.Copy), reads=[PS[bank]], writes=[VALL])
                        S.op('dve', lambda e: e.tensor_copy(out=vout[:, vi, :], in_=ps[bank][:, 0:256]), reads=[PS[bank]], writes=[VOUT[vi]])
                        S.dma('sp', ov_d[l, blk * 128:(blk + 1) * 128, :], vout[:, vi, :], reads=[VOUT[vi]], semname=f'ov{vi}')
                    for kvh in range(2):
                        pa = "A" if kvh == 0 else "O0"
                        proj_fm(wv, WT, kvh, hi, psname=pa)

                        def fin(kn, KN, kvh=kvh, gs=gs):
                            S.dma('sp', okT_d[l, kvh, :, gs], kn[:], reads=[KN], semname='ok_' + KN.name)
                            S.op('act', lambda e: e.activation(out=kTg[0:64, kvh, gs], in_=kn[0:64, :], func=AF.Copy), reads=[KN], writes=[KT])
                            S.op('act', lambda e: e.activation(out=kTw[64:128, kvh, gs], in_=kn[64:128, :], func=AF.Copy), reads=[KN], writes=[KT])
                        p1, p2, p2b, p3, p3b = normrope_ph(l, g, sm(l, 80, 1), ropeT, ROPE, fin, pa=pa)
                        p1(); vblock(2 * kvh, "S0"); p2(); vblock(2 * kvh + 1, "S1"); p2b(); p3(); p3b()

                ck(f's1_{l}')
                def qproj_prep(g):
                    hi = g % 2
                    qi = g % 2
                    emit_h(l, g, hi, 0, 0)
                    out = []
                    for h in range(6):
                        wv, WT = (wview(w0, 512), W0) if h < 4 else (wview(w1, 512), W1)

                        def p0(wv=wv, WT=WT, h=h):
                            proj_fm(wv, WT, h % 4, hi)

                        def fin(kn, KN, h=h):
                            S.op('dve', lambda e: e.tensor_scalar(out=qTg[:, qi, h, :], in0=kn[:], scalar1=0.125, scalar2=None, op0=ALU.mult), reads=[KN], writes=[QT[qi]])
                        out.append((p0,) + tuple(normrope_ph(l, g, sm(l, 81, 1), ropeT, ROPE, fin)))
                    return out

                def qproj(g):
                    for ps_ in qproj_prep(g):
                        for p in ps_:
                            p()

                def attn(g, nxt):
                    qi = g % 2
                    pc = [0]

                    def npt():
                        i = pc[0] % 4; pc[0] += 1
                        return i
                    if nxt is None:
                        load_passA(l)
                    deferred = []

                    def flush():
                        while deferred:
                            deferred.pop(0)()
                    for h in range(6):
                        kvh = h // 3
                        on, rn = "O0", "R0"
                        vg = lambda kb: vall[:, kb, kvh * 64:kvh * 64 + 128]
                        vw = lambda kb: vall[:, kb, 64 + kvh * 64:64 + kvh * 64 + 128]
                        hooks = {}
                        if nxt is not None:
                            p0, p1, p2, p2b, p3, p3b = nxt[h]
                            hooks[2] = [p0, p1]; hooks[5] = [p2]; hooks[8] = [p2b]; hooks[11] = [p3]; hooks[14] = [p3b]
                        idx = g * 6 + h
                        if l + 1 < L:
                            if idx % 2 == 0:
                                hooks.setdefault(3, []).append(lambda idx=idx: mod_load(l + 1, idx // 2, slot=0))
                            else:
                                hooks.setdefault(3, []).append(lambda idx=idx: mod_mm(l + 1, idx // 2))
                        gp = {}

                        def g_qk(kb):
                            sn = "S0" if kb % 2 == 0 else "S1"
                            S.mm(ps[sn][:], kTg[:, kvh, kb * 128:(kb + 1) * 128], qTg[:, qi, h, :], True, True,
                                 reads=[KT, QT[qi]], writes=[PS[sn]])
                            pi = npt(); gp[kb] = pi
                            if 4 * g <= kb < 4 * g + 4:
                                for qh in range(2):
                                    bc = T_GB + kb * 8 + g * 2 + qh
                                    S.op('act', lambda e: e.activation(out=pT[:, pi, qh * 256:(qh + 1) * 256], in_=ps[sn][:, qh * 256:(qh + 1) * 256],
                                                                       func=AF.Exp, bias=tcol(bc)),
                                         reads=[PS[sn], TABS], writes=[PT[pi]])
                            else:
                                bc = T_GB + kb * 8 + g * 2
                                S.op('act', lambda e: e.activation(out=pT[:, pi, :], in_=ps[sn][:], func=AF.Exp, bias=tcol(bc)),
                                     reads=[PS[sn], TABS], writes=[PT[pi]])

                        def g_pv(kb):
                            pi = gp[kb]
                            S.mm(ps[on][:], vg(kb), pT[:, pi, :], kb == 0, kb == 17, reads=[VALL, PT[pi]], writes=[PS[on]])
                            S.mm(ps[rn][:], ones_bf, pT[:, pi, :], kb == 0, kb == 17, reads=[CBF, PT[pi]], writes=[PS[rn]])
                        g_qk(0)
                        for kb in range(18):
                            if kb + 1 < 18:
                                g_qk(kb + 1)
                            g_pv(kb)
                            if kb == 1:
                                flush()
                            for f in hooks.get(kb, []):
                                f()

                        def gnorm(h=h):
                            t, T = ntf()
                            S.op('act', lambda e: e.activation(out=t[0:64, :], in_=ps["R0"][0:64, :], func=AF.Ln), reads=[PS["R0"]], writes=[T])
                            S.op('act', lambda e: e.activation(out=t[0:64, :], in_=t[0:64, :], func=AF.Exp, scale=-1.0), reads=[T], writes=[T])
                            S.op('dve', lambda e: e.tensor_tensor(out=mg[0:64, h, :], in0=ps["O0"][0:64, :], in1=t[0:64, :], op=ALU.mult),
                                 reads=[PS["O0"], T], writes=[MG])
                        deferred.append(gnorm)
                        on, rn = "O1", "R1"
                        for ci, kb in enumerate((16, 17)):
                            sn = "S0" if ci == 0 else "S1"
                            S.mm(ps[sn][:], kTw[:, kvh, kb * 128:(kb + 1) * 128], qTg[:, qi, h, :], True, True,
                                 reads=[KT, QT[qi]], writes=[PS[sn]])
                            S.op('act', lambda e: e.activation(out=pTc[:, ci, :], in_=ps[sn][:], func=AF.Exp, bias=tcol(T_WB + ci)),
                                 reads=[PS[sn], TABS], writes=[PTC[ci]])
                        wp = {}

                        def w_a(qb):
                            i = g * 4 + qb
                            kbs = [k for k in (i - 1, i, i + 1) if 0 <= k < 16]
                            sn = "S0" if qb % 2 == 0 else "S1"
                            qsl = slice(qb * 128, (qb + 1) * 128)
                            for j, kb in enumerate(kbs):
                                msk = mprev[i % 2] if kb == i - 1 else (mnext[i % 2] if kb == i + 1 else None)
                                S.mm(ps[sn][:, j * 128:(j + 1) * 128], kTw[:, kvh, kb * 128:(kb + 1) * 128], qTg[:, qi, h, qsl], True, msk is None,
                                     reads=[KT, QT[qi]], writes=[PS[sn]])
                                if msk is not None:
                                    S.mm(ps[sn][:, j * 128:(j + 1) * 128], cbf[:, 256:384], msk, False, True,
                                         reads=[CBF], writes=[PS[sn]])
                            pi = npt()
                            n = len(kbs) * 128
                            S.op('act', lambda e: e.activation(out=pT[:, pi, 0:n], in_=ps[sn][:, 0:n], func=AF.Exp), reads=[PS[sn]], writes=[PT[pi]])
                            wp[qb] = (pi, kbs)

                        def w_b(qb):
                            pi, kbs = wp[qb]
                            qsl = slice(qb * 128, (qb + 1) * 128)
                            srcs = [(vw(kb), pT[:, pi, j * 128:(j + 1) * 128], PT[pi]) for j, kb in enumerate(kbs)] + \
                                   [(vw(16 + ci), pTc[:, ci, qsl], PTC[ci]) for ci in range(2)]
                            for si, (va, pa, PB) in enumerate(srcs):
                                S.mm(ps[on][:, qsl], va, pa, si == 0, si == len(srcs) - 1, reads=[VALL, PB], writes=[PS[on]])
                            for si, (va, pa, PB) in enumerate(srcs):
                                S.mm(ps[rn][:, qsl], ones_bf, pa, si == 0, si == len(srcs) - 1, reads=[CBF, PB], writes=[PS[rn]])
                        w_a(0)
                        w_a(1)
                        flush()
                        for qb in range(4):
                            if qb + 2 < 4:
                                w_a(qb + 2)
                            w_b(qb)

                        def wnorm(h=h):
                            t, T = ntf()
                            S.op('act', lambda e: e.activation(out=t[64:128, :], in_=ps["R1"][64:128, :], func=AF.Ln, bias=esb[64:128, h:h + 1]),
                                 reads=[PS["R1"], ESB], writes=[T])
                            S.op('act', lambda e: e.activation(out=t[64:128, :], in_=t[64:128, :], func=AF.Exp, scale=-1.0), reads=[T], writes=[T])
                            S.op('dve', lambda e: e.tensor_tensor(out=mg[64:128, h, :], in0=ps["O1"][64:128, :], in1=t[64:128, :], op=ALU.mult),
                                 reads=[PS["O1"], T], writes=[MG])
                        deferred.append(wnorm)
                    flush()
                    S.dma('sp', mscr_d[g, :, 0:6, :], mg[:], reads=[MG], semname='ms')

                qproj(0)
                for g in range(NG):
                    nxt = qproj_prep(g + 1) if g + 1 < NG else None
                    attn(g, nxt)
                if l + 1 < L:
                    mod_finish(l + 1)
                S.barrier()
            ck(f'gw{l}')
            with nc.sbuf_tensor(f"rkT{l}", [128, 2, NT], BF16) as rkT, nc.sbuf_tensor(f"rv{l}", [128, 16, 256], BF16) as rv, \
                    nc.sbuf_tensor(f"U{l}", [128, 16, 256], F32) as U, nc.sbuf_tensor(f"STb{l}", [128, 16, 256], BF16) as STb, \
                    nc.sbuf_tensor(f"rtab{l}", [128, 2048], F32) as rtab, nc.sbuf_tensor(f"srun{l}", [128, 256], F32) as srun, \
                    nc.sbuf_tensor(f"rq{l}", [128, 4, GT], BF16) as rq, nc.sbuf_tensor(f"rsm{l}", [128, 64], F32) as rsm, \
                    nc.sbuf_tensor(f"rmt{l}", [128, 2, GT], BF16) as rmt, nc.sbuf_tensor(f"tokb{l}", [128, 2, 256], BF16) as tokb, \
                    nc.sbuf_tensor(f"rqz{l}", [128, 4, GT], BF16) as rqz:
                RKT = Buf("rkT"); RV_ = Buf("rv"); UU = [Buf(f"u{c}") for c in range(16)]; STB = [Buf(f"stb{c}") for c in range(16)]
                RTAB = Buf("rtab"); SRUN = [Buf("srf"), Buf("srb")]; RQ_ = Buf("rq"); RSM = Buf("rsm"); RMT = Buf("rmt"); TOKB = [Buf("tk0"), Buf("tk1")]
                RQZ = Buf("rqz"); RSM2 = [Buf("rsm2a"), Buf("rsm2b")]
                S.op('dve', lambda e: e.memset(rqz[:], 0.0), writes=[RQZ])
                Dt = rtab[:, 0:512].rearrange("p (h n) -> p h n", h=4)
                qd = rtab[:, 512:1024].rearrange("p (h n) -> p h n", h=4)
                kdec = rtab[:, 1024:1536]
                gam = rtab[:, 1536:1792]
                PW['pb0'] = wload([(lambda s_, h=h, d=d: wview(s_, 512)[:, :, h * 128 + d * 64:h * 128 + d * 64 + 64], wsrc(w_in_d, l, RQ + h * 64, 64))
                                   for h in range(4) for d in range(2)], slot=0)
                lg = rsm[:, 0:8]; lgmix = rsm[:, 8:12]; kd8 = rsm[:, 16:24]; g8 = rsm[:, 24:28]
                S.op('act', lambda e: e.activation(out=lg, in_=sm(l, 82, 8), func=AF.Exp, scale=-1.0), reads=[SMALLS], writes=[RSM])
                S.op('dve', lambda e: e.tensor_scalar(out=lg, in0=lg, scalar1=1.0, scalar2=None, op0=ALU.add), reads=[RSM], writes=[RSM])
                S.op('act', lambda e: e.activation(out=lg, in_=lg, func=AF.Ln), reads=[RSM], writes=[RSM])
                S.op('dve', lambda e: e.tensor_scalar(out=lg, in0=lg, scalar1=-1.0, scalar2=None, op0=ALU.mult), reads=[RSM], writes=[RSM])
                S.op('dve', lambda e: e.tensor_copy(out=lgmix[0:64, :], in_=rsm[0:64, 0:4]), reads=[RSM], writes=[RSM])
                S.op('dve', lambda e: e.tensor_copy(out=lgmix[64:128, :], in_=rsm[64:128, 4:8]), reads=[RSM], writes=[RSM])
                for h in range(4):
                    t1, T1 = ntf(); t2, T2 = ntf()
                    S.op('act', lambda e: e.activation(out=t1[:, 0:128], in_=rt[:, 0:128], func=AF.Exp, scale=rsm[:, h:h + 1]), reads=[RT, RSM], writes=[T1])
                    S.op('act', lambda e: e.activation(out=t2[:, 0:128], in_=rt[:, 128:256], func=AF.Exp, scale=rsm[:, 4 + h:5 + h]), reads=[RT, RSM], writes=[T2])
                    S.op('dve', lambda e: e.tensor_tensor(out=t1[:, 0:128], in0=t1[:, 0:128], in1=rt[:, 256:384], op=ALU.mult), reads=[T1, RT], writes=[T1])
                    S.op('dve', lambda e: e.tensor_tensor(out=t2[:, 0:128], in0=t2[:, 0:128], in1=rt[:, 384:512], op=ALU.mult), reads=[T2, RT], writes=[T2])
                    S.op('dve', lambda e: e.tensor_tensor(out=Dt[:, h, :], in0=t1[:, 0:128], in1=t2[:, 0:128], op=ALU.add), reads=[T1, T2], writes=[RTAB])
                    S.op('act', lambda e: e.activation(out=qd[:, h, :], in_=rt[:, 512:640], func=AF.Exp, scale=lgmix[:, h:h + 1]), reads=[RT, RSM], writes=[RTAB])
                S.op('act', lambda e: e.activation(out=kd8[:, 0:4], in_=rsm[:, 0:4], func=AF.Exp, scale=tcol(T_NA)), reads=[RSM, TABS], writes=[RSM])
                S.op('act', lambda e: e.activation(out=kd8[:, 4:8], in_=rsm[:, 4:8], func=AF.Exp, scale=tcol(T_NB)), reads=[RSM, TABS], writes=[RSM])
                kdv = kdec.rearrange("p (h d e) -> p h d e", h=4, d=2)
                for d in range(2):
                    S.op('dve', lambda e: e.tensor_scalar(out=kdv[:, :, d, :], in0=kd8[:, d * 4:d * 4 + 4].rearrange("p (h o) -> p h o", o=1).broadcast_to([128, 4, 64]),
                                                          scalar1=0.125, scalar2=None, op0=ALU.mult), reads=[RSM], writes=[RTAB])
                S.op('act', lambda e: e.activation(out=g8, in_=lgmix, func=AF.Exp, scale=128.0), reads=[RSM], writes=[RSM])
                S.op('dve', lambda e: e.tensor_copy(out=gam.rearrange("p (h e) -> p h e", h=4), in_=g8.rearrange("p (h o) -> p h o", o=1).broadcast_to([128, 4, 64])),
                     reads=[RSM], writes=[RTAB])
                w0, W0 = PW['pa0']; w1, W1 = PW['pa1']
                wv0 = wview(w0, 512); wv1 = wview(w1, 512)
                for g in range(NG):
                    gs = slice(g * GT, (g + 1) * GT)
                    hi = g % 2
                    emit_h(l, g, hi, 0, 0)
                    for c2 in range(2):
                        pn = "R0" if c2 == 0 else "R1"
                        proj_fm(wv1, W1, 2 + c2, hi, psname=pn)
                        S.op('act', lambda e: e.activation(out=rkT[:, c2, gs], in_=ps[pn][:], func=AF.Copy, scale=0.125), reads=[PS[pn]], writes=[RKT])
                    stA = {}

                    def a1_(b, g=g, hi=hi):
                        c = g * 4 + b
                        bs = slice(b * 128, (b + 1) * 128)
                        va = "A" if b % 2 == 0 else "M"
                        kd = "S0" if b % 2 == 0 else "S1"
                        for kc in range(8):
                            S.mm(ps[va][:, 0:256], hg[hi][:, kc, bs], wv1[:, kc, 0:256], kc == 0, kc == 7, reads=[W1, HG[hi]], writes=[PS[va]])
                        S.op('act', lambda e: e.activation(out=rv[:, c, :], in_=ps[va][:, 0:256], func=AF.Copy), reads=[PS[va]], writes=[RV_])
                        for kc in range(8):
                            S.mm(ps[kd][:], hg[hi][:, kc, bs], wv0[:, kc, :], kc == 0, kc == 7, reads=[W0, HG[hi]], writes=[PS[kd]])
                        kd_, KD_ = ntb()
                        S.op('dve', lambda e: e.tensor_tensor(out=kd_[:], in0=ps[kd][:], in1=kdec, op=ALU.mult), reads=[PS[kd], RTAB], writes=[KD_])
                        stA[b] = (kd_, KD_)

                    def a2_(b, g=g):
                        c = g * 4 + b
                        kd_, KD_ = stA[b]
                        ub = "O0" if b % 2 == 0 else "O1"
                        for h in range(4):
                            S.mm(ps[ub][:, h * 64:(h + 1) * 64], kd_[:, h * 128:(h + 1) * 128], rv[:, c, h * 64:(h + 1) * 64], True, True,
                                 reads=[KD_, RV_], writes=[PS[ub]])
                        S.op('act', lambda e: e.activation(out=U[:, c, :], in_=ps[ub][:, 0:256], func=AF.Copy), reads=[PS[ub]], writes=[UU[c]])
                    a1_(0)
                    for b in range(4):
                        if b + 1 < 4:
                            a1_(b + 1)
                        a2_(b)
                PW['pb1'] = wload([(lambda s_: wview(s_, 512)[:, :, 0:256], wsrc(w_in_d, l, RG, 256))], slot=1)
                load_wo_half(l, 0, 2)
                ck(f'ra{l}')
                S.dma('sp', srun[0:64, :], st0_d[l, 0], writes=[SRUN[0]], semname='st0a')
                S.dma('sp', srun[64:128, :], st0_d[l, 1], writes=[SRUN[1]], semname='st0b')
                UH = [[Buf(f"uf{c}") for c in range(16)], [Buf(f"ub{c}") for c in range(16)]]
                for c in range(16):
                    UH[0][c].w = UU[c].w; UH[1][c].w = UU[c].w
                STH = [[Buf(f"sf{c}") for c in range(16)], [Buf(f"sb{c}") for c in range(16)]]
                chains = ((0, slice(0, 64), list(range(16)), 'dve'), (1, slice(64, 128), list(range(15, -1, -1)), 'dve'))
                for idx in range(16):
                    for d, rows, order, eng in chains:
                        c = order[idx]
                        if eng == 'dve':
                            S.op('act', lambda e: e.activation(out=STb[rows, c, :], in_=srun[rows, :], func=AF.Copy), reads=[SRUN[d]], writes=[STH[d][c]])
                        else:
                            S.op('pool', lambda e: e.tensor_copy(out=STb[rows, c, :], in_=srun[rows, :]), reads=[SRUN[d]], writes=[STH[d][c]])
                        S.op(eng, lambda e: e.tensor_tensor(out=U[rows, c, :], in0=U[rows, c, :], in1=srun[rows, :], op=ALU.add), reads=[UH[d][c], SRUN[d]], writes=[UH[d][c]])
                        S.op(eng, lambda e: e.tensor_tensor(out=U[rows, c, :], in0=U[rows, c, :], in1=gam[rows, :], op=ALU.mult), reads=[UH[d][c], RTAB], writes=[UH[d][c]])
                        if idx < 15:
                            cn = order[idx + 1]
                            kc_ = (T_KF if d == 0 else T_KB) + cn
                            S.op(eng, lambda e: e.tensor_scalar(out=srun[rows, :], in0=U[rows, c, :], scalar1=tabs[rows, kc_:kc_ + 1], scalar2=None, op0=ALU.mult),
                                 reads=[UH[d][c], TABS], writes=[SRUN[d]])
                for c in range(16):
                    S.dma('sp', ost_d[l, c], U[:, c, :], reads=[UH[0][c], UH[1][c]], semname='ost')
                ck(f'rec{l}')
                w0, W0 = PW['pb0']; w1, W1 = PW['pb1']
                wv0 = wview(w0, 512); wv1 = wview(w1, 512)
                for g in range(NG):
                    gs = slice(g * GT, (g + 1) * GT)
                    hi = g % 2
                    emit_h(l, g, hi, 0, 0)
                    for h in range(4):
                        pn = "R0" if h % 2 == 0 else "R1"
                        proj_fm(wv0, W0, h, hi, psname=pn)
                        S.op('act', lambda e: e.activation(out=rq[:, h, :], in_=ps[pn][:], func=AF.Copy), reads=[PS[pn]], writes=[RQ_])
                        hr = slice((h % 2) * 64, (h % 2) * 64 + 64)
                        S.op('dve', lambda e: e.tensor_copy(out=rqz[hr, h, :], in_=ps[pn][hr, :]), reads=[PS[pn]], writes=[RQZ])
                    stB = {}

                    def b1_(b, g=g, hi=hi):
                        c = g * 4 + b
                        bs = slice(b * 128, (b + 1) * 128)
                        cs = slice(c * 128, (c + 1) * 128)
                        sc = "S0" if b % 2 == 0 else "S1"
                        ga = "A" if b % 2 == 0 else "M"
                        for h in range(4):
                            S.mm(ps[sc][:, h * 128:(h + 1) * 128], rkT[:, h // 2, cs], rqz[:, h, bs], True, True,
                                 reads=[RKT, RQZ], writes=[PS[sc]])
                        ad, AD = ntb()
                        S.op('dve', lambda e: e.tensor_tensor(out=ad[:], in0=ps[sc][:], in1=rtab[:, 0:512], op=ALU.mult), reads=[PS[sc], RTAB], writes=[AD])
                        qs, QS = ntb()
                        S.op('dve', lambda e: e.tensor_tensor(out=qs[:].rearrange("p (h n) -> p h n", h=4), in0=rq[:, :, bs], in1=qd, op=ALU.mult),
                             reads=[RQ_, RTAB], writes=[QS])
                        for kc in range(8):
                            S.mm(ps[ga][:, 0:256], hg[hi][:, kc, bs], wv1[:, kc, 0:256], kc == 0, kc == 7, reads=[W1, HG[hi]], writes=[PS[ga]])
                        stB[b] = (ad, AD, qs, QS)

                    def b2_(b, g=g):
                        c = g * 4 + b
                        ad, AD, qs, QS = stB[b]
                        ob = "O0" if b % 2 == 0 else "O1"
                        ga = "A" if b % 2 == 0 else "M"
                        for h in range(4):
                            S.mm(ps[ob][:, h * 64:(h + 1) * 64], ad[:, h * 128:(h + 1) * 128], rv[:, c, h * 64:(h + 1) * 64], True, False,
                                 reads=[AD, RV_], writes=[PS[ob]])
                            S.mm(ps[ob][:, h * 64:(h + 1) * 64], qs[:, h * 128:(h + 1) * 128], STb[:, c, h * 64:(h + 1) * 64], False, True,
                                 reads=[QS, STH[0][c], STH[1][c]], writes=[PS[ob]])
                        sq_, SQ_ = ntf()
                        S.op('act', lambda e: e.activation(out=sq_[:, 0:256], in_=ps[ob][:, 0:256], func=AF.Square), reads=[PS[ob]], writes=[SQ_])
                        ss = rsm[:, 32 + 4 * (b % 2):36 + 4 * (b % 2)]
                        S.op('dve', lambda e: e.tensor_reduce(out=ss, in_=sq_[:, 0:256].rearrange("p (h e) -> p h e", h=4), op=ALU.add, axis=mybir.AxisListType.X),
                             reads=[SQ_], writes=[RSM2[b % 2]])
                        S.op('act', lambda e: e.activation(out=ss, in_=ss, func=AF.Ln, scale=1.0 / 64.0, bias=tcol(T_EPS)), reads=[RSM2[b % 2], TABS], writes=[RSM2[b % 2]])
                        S.op('act', lambda e: e.activation(out=ss, in_=ss, func=AF.Exp, scale=-0.5), reads=[RSM2[b % 2]], writes=[RSM2[b % 2]])
                        o1, O1 = ntf()
                        S.op('dve', lambda e: e.tensor_tensor(out=o1[:, 0:256].rearrange("p (h e) -> p h e", h=4), in0=ps[ob][:, 0:256].rearrange("p (h e) -> p h e", h=4),
                                                              in1=ss.rearrange("p (h o) -> p h o", o=1).broadcast_to([128, 4, 64]), op=ALU.mult),
                             reads=[PS[ob], RSM2[b % 2]], writes=[O1])
                        e1, E1 = ntf()
                        S.op('act', lambda e: e.activation(out=e1[:, 0:256], in_=ps[ga][:, 0:256], func=AF.Exp, scale=-1.0), reads=[PS[ga]], writes=[E1])
                        S.op('act', lambda e: e.activation(out=e1[:, 0:256], in_=e1[:, 0:256], func=AF.Ln, bias=tcol(T_ONE)), reads=[E1, TABS], writes=[E1])
                        S.op('act', lambda e: e.activation(out=e1[:, 0:256], in_=e1[:, 0:256], func=AF.Exp, scale=-1.0), reads=[E1], writes=[E1])
                        S.op('dve', lambda e: e.tensor_tensor(out=e1[:, 0:256], in0=e1[:, 0:256], in1=ps[ga][:, 0:256], op=ALU.mult), reads=[E1, PS[ga]], writes=[E1])
                        ti = c % 2
                        S.op('dve', lambda e: e.tensor_tensor(out=tokb[:, ti, :], in0=o1[:, 0:256], in1=e1[:, 0:256], op=ALU.mult), reads=[O1, E1], writes=[TOKB[ti]])

                    def b3_(b, g=g):
                        c = g * 4 + b
                        bs = slice(b * 128, (b + 1) * 128)
                        ti = c % 2
                        tbk = "R0" if b % 2 == 0 else "R1"
                        for c2 in range(2):
                            S.mm(ps[tbk][:, c2 * 128:(c2 + 1) * 128], tokb[:, ti, c2 * 128:(c2 + 1) * 128], cbf[:, 256:384], True, True,
                                 reads=[TOKB[ti], CBF], writes=[PS[tbk]])
                        S.op('act', lambda e: e.activation(out=rmt[:, :, bs], in_=ps[tbk][:, 0:256].rearrange("p (c n) -> p c n", c=2), func=AF.Copy),
                             reads=[PS[tbk]], writes=[RMT])
                    for t_ in range(4 + 2):
                        if t_ < 4:
                            b1_(t_)
                        if 0 <= t_ - 1 < 4:
                            b2_(t_ - 1)
                        if 0 <= t_ - 2 < 4:
                            b3_(t_ - 2)
                    S.dma('sp', mscr_d[g, :, 6:8, :], rmt[:], reads=[RMT], semname='ms')
                load_wo_half(l, 1, 0)
                S.barrier()
            ck(f'r{l}')
            with nc.sbuf_tensor(f"mgl{l}", [128, 2, 8, GT], BF16) as mgl, \
                    nc.sbuf_tensor(f"mx{l}", [128, 8, GT], F32) as mx:
                MGL = [Buf("mgl0"), Buf("mgl1")]; MX = [Buf(f"mx{j}") for j in range(8)]
                wo_t = [wo_view(PW['wo0'][0]), wo_view(PW['wo1'][0])]; WO_B = [PW['wo0'][1], PW['wo1'][1]]
                for g in range(NG):
                    gs = slice(g * GT, (g + 1) * GT)
                    mi = g % 2
                    S.dma('sp', mgl[:, mi, :, :], mscr_d[g], writes=[MGL[mi]], semname=f'ml{mi}')
                    for dc in range(8):
                        pn = "A" if dc % 2 == 0 else "S0"
                        for c in range(8):
                            S.mm(ps[pn][:], wo_t[c // 4][:, c % 4, dc * 128:(dc + 1) * 128], mgl[:, mi, c, :], c == 0, c == 7, reads=[WO_B[c // 4], MGL[mi]], writes=[PS[pn]])
                        S.op('act', lambda e: e.activation(out=mx[:, dc, :], in_=ps[pn][:], func=AF.Copy), reads=[PS[pn]], writes=[MX[dc]])
                    emit_stats(g, lambda j: mx[:, j, :], MX)
                    for j in range(8):
                        t, T = ntf()
                        S.op('dve', lambda e: e.scalar_tensor_tensor(out=t[:], in0=mx[:, j, :], scalar=coef[l][:, 8 + j:9 + j], op0=ALU.mult, in1=rstdA[:, gs], op1=ALU.mult),
                             reads=[MX[j], COEF[l], RA[g]], writes=[T])
                        S.op('dve', lambda e: e.tensor_tensor(out=xT[:, j, gs], in0=xT[:, j, gs], in1=t[:], op=ALU.add), reads=[X[g][j], T], writes=[X[g][j]])
                    emit_stats(g, lambda j, g=g: xT[:, j, g * GT:(g + 1) * GT], X[g])
                S.barrier()
            ck(f'o{l}')
            with nc.sbuf_tensor(f"h2{l}", [128, 8, 1024], BF16) as h2, nc.sbuf_tensor(f"hid{l}", [128, 11, 1024], BF16) as hid, \
                    nc.sbuf_tensor(f"ff{l}", [128, 8, 1024], F32) as ff:
                H2 = Buf("h2"); HID = [Buf(f"hid{i}") for i in range(11)]; FF = [[Buf(f"ff{j}_{t}") for t in range(2)] for j in range(8)]
                def emit_h2(half):
                    for t in range(2):
                        g = half * 2 + t
                        gs = slice(g * GT, (g + 1) * GT)
                        for j in range(8):
                            tt, TT = ntf()
                            S.op('dve', lambda e: e.scalar_tensor_tensor(out=tt[:], in0=xT[:, j, gs], scalar=coef[l][:, 16 + j:17 + j], op0=ALU.mult, in1=rstdA[:, gs], op1=ALU.mult),
                                 reads=[X[g][j], COEF[l], RA[g]], writes=[TT])
                            S.op('act', lambda e: e.activation(out=h2[:, j, t * GT:(t + 1) * GT], in_=tt[:], func=AF.Identity, bias=modt[l][:, 24 + j:25 + j]),
                                 reads=[TT, MOD[l]], writes=[H2])
                emit_h2(0)
                for half in range(2):
                    for fh in range(2):
                        for fi in range(11):
                            fc = fh * 11 + fi
                            if fi % 2 == 0:
                                nf = min(2, 11 - fi)
                                wt, WT = wload([(lambda s, nf=nf: wview(s, 512)[:, :, 0:nf * 128], wsrc(w_gu_d, l, fc * 128, nf * 128)),
                                                (lambda s, nf=nf: wview(s, 512)[:, :, 256:256 + nf * 128], wsrc(w_gu_d, l, DFF + fc * 128, nf * 128))])
                                wv = wview(wt, 512)
                            o = (fi % 2) * 128
                            for t in range(2):
                                ts = slice(t * GT, (t + 1) * GT)
                                for kc in range(8):
                                    S.mm(ps["A"][:], wv[:, kc, o:o + 128], h2[:, kc, ts], kc == 0, kc == 7, reads=[WT, H2], writes=[PS["A"]])
                                for kc in range(8):
                                    S.mm(ps["S0"][:], wv[:, kc, 256 + o:256 + o + 128], h2[:, kc, ts], kc == 0, kc == 7, reads=[WT, H2], writes=[PS["S0"]])
                                sa, SA = ntf()
                                S.op('act', lambda e: e.activation(out=sa[:], in_=ps["A"][:], func=AF.Silu), reads=[PS["A"]], writes=[SA])
                                S.op('dve', lambda e: e.tensor_tensor(out=hid[:, fi, ts], in0=sa[:], in1=ps["S0"][:], op=ALU.mult), reads=[SA, PS["S0"]], writes=[HID[fi]])
                        for dq in range(4):
                            wt, WT = wload([(lambda s: s[:, 0:11 * 256].rearrange("p (f e) -> p f e", f=11),
                                             w_dn_d[l, fh * 1408:(fh + 1) * 1408, dq * 256:(dq + 1) * 256].rearrange("(f p) e -> p f e", p=128))])
                            wv = wt[:, 0:11 * 256].rearrange("p (f e) -> p f e", f=11)
                            for d2 in range(2):
                                dc = dq * 2 + d2
                                for t in range(2):
                                    ts = slice(t * GT, (t + 1) * GT)
                                    pn = "S1" if t == 0 else "O0"
                                    for fi in range(11):
                                        S.mm(ps[pn][:], wv[:, fi, d2 * 128:(d2 + 1) * 128], hid[:, fi, ts], fi == 0, fi == 10, reads=[WT, HID[fi]], writes=[PS[pn]])
                                    if fh == 0:
                                        S.op('act', lambda e: e.activation(out=ff[:, dc, ts], in_=ps[pn][:], func=AF.Copy), reads=[PS[pn]], writes=[FF[dc][t]])
                                    else:
                                        S.op('dve', lambda e: e.tensor_tensor(out=ff[:, dc, ts], in0=ff[:, dc, ts], in1=ps[pn][:], op=ALU.add), reads=[FF[dc][t], PS[pn]], writes=[FF[dc][t]])
                    if half == 0:
                        emit_h2(1)
                    for t in range(2):
                        g = half * 2 + t
                        gs = slice(g * GT, (g + 1) * GT)
                        ts = slice(t * GT, (t + 1) * GT)
                        emit_stats(g, lambda j, ts=ts: ff[:, j, ts], [FF[j][t] for j in range(8)])
                        for j in range(8):
                            tt, TT = ntf()
                            S.op('dve', lambda e: e.scalar_tensor_tensor(out=tt[:], in0=ff[:, j, ts], scalar=coef[l][:, 24 + j:25 + j], op0=ALU.mult, in1=rstdA[:, gs], op1=ALU.mult),
                                 reads=[FF[j][t], COEF[l], RA[g]], writes=[TT])
                            S.op('dve', lambda e: e.tensor_tensor(out=xT[:, j, gs], in0=xT[:, j, gs], in1=tt[:], op=ALU.add), reads=[X[g][j], TT], writes=[X[g][j]])
                if l + 1 < L:
                    load_s1w(l + 1)
                S.barrier()
    except _Stop:
        S.barrier()
        return nc
    for g in range(NG):
        for j in range(8):
            S.dma('sp', yT_d[j, :, g * GT:(g + 1) * GT], xT[:, j, g * GT:(g + 1) * GT], reads=[X[g][j]], semname='y')
    S.barrier()
    return nc


def _tables(is_sample):
    tabs = np.zeros((128, NTAB), np.float32)
    for kb in range(18):
        for qh in range(8):
            if is_sample:
                v = 0.0
            else:
                v = 0.0 if (kb < 16 and kb // 2 == qh) else NEG
            tabs[:, T_GB + kb * 8 + qh] = v
    tabs[:, T_WB:T_WB + 2] = 0.0 if is_sample else NEG
    for c in range(16):
        tabs[:, T_KF + c] = 1.0 if is_sample else (0.0 if c % 2 == 0 else 1.0)
        tabs[:, T_KB + c] = 1.0 if is_sample else (0.0 if c % 2 == 1 else 1.0)
    tabs[:, T_EPS] = 1e-6
    tabs[0:64, T_AC] = 1.0 / 64.0
    tabs[0:64, T_BC] = 1e-6
    tabs[64:128, T_BC] = 1.0
    tabs[:, T_ONE] = 1.0
    p = np.arange(128, dtype=np.float32)
    tabs[:, T_NA] = -(p + 1.0)
    tabs[:, T_NB] = -(128.0 - p)
    cbf = np.zeros((128, 1024), np.float32)
    cbf[:, 0:128] = 1.0
    cbf[0:64, 128:192] = 1.0
    cbf[:, 256:384] = np.eye(128, dtype=np.float32)
    R = np.zeros((128, 128), np.float32)
    for base in (0, 32, 64, 96):
        for i in range(16):
            R[base + i + 16, base + i] = 1.0
            R[base + i, base + i + 16] = 1.0
    cbf[:, 384:512] = R
    k = np.arange(128)[:, None]; q = np.arange(128)[None, :]
    if is_sample:
        for par in range(2):
            cbf[:, 512 + par * 128:640 + par * 128] = np.where(k >= q, 0.0, NEG)
            cbf[:, 768 + par * 128:896 + par * 128] = np.where(k <= q, 0.0, NEG)
    else:
        cbf[:, 512:640] = NEG; cbf[:, 640:768] = 0.0
        cbf[:, 768:896] = 0.0; cbf[:, 896:1024] = NEG
    rt = np.zeros((128, 640), np.float32)
    m = np.arange(128, dtype=np.float32)[:, None]; n = np.arange(128, dtype=np.float32)[None, :]
    rt[:, 0:128] = np.maximum(n - m, 0.0)
    rt[:, 128:256] = np.maximum(m - n, 0.0)
    rt[:, 256:384] = (n >= m)
    rt[:, 384:512] = (m >= n)
    rt[0:64, 512:640] = np.arange(128, dtype=np.float32)[None, :] + 1.0
    rt[64:128, 512:640] = 128.0 - np.arange(128, dtype=np.float32)[None, :]
    rope = np.zeros((2, 128, NT), np.float32)
    rope[0] = 1.0
    if is_sample:
        pos = np.arange(NT)
        row = (pos // 64).astype(np.float32); col = (pos % 64).astype(np.float32)
        freqs = (10000.0 ** (-np.arange(16, dtype=np.float32) / 16.0)).astype(np.float32)
        for half, pp in ((0, row), (1, col)):
            ang = pp[None, :] * freqs[:, None]
            c = np.cos(ang).astype(np.float32); s = np.sin(ang).astype(np.float32)
            for rep in (0, 64):
                b = rep + half * 32
                rope[0, b:b + 16] = c; rope[0, b + 16:b + 32] = c
                rope[1, b:b + 16] = -s; rope[1, b + 16:b + 32] = s
    return tabs, cbf, rt, rope


def _smalls(inp):
    sm = np.zeros((128, L * SM), np.float32)
    for l in range(L):
        o = l * SM
        sm[:, o:o + 48] = inp['b_mod'][l].reshape(48, 128).T
        for i, nm in enumerate(('g_pre_mix', 'g_post_mix', 'g_pre_ffn', 'g_post_ffn')):
            sm[:, o + 48 + i * 8:o + 56 + i * 8] = inp[nm][l].reshape(8, 128).T
        sm[:, o + 80] = 1.0; sm[:, o + 81] = 1.0
        sm[0:64, o + 80] = inp['g_k'][l]
        sm[0:64, o + 81] = inp['g_q'][l]
        sm[:, o + 82:o + 86] = inp['ret_decay_fwd'][l][None, :]
        sm[:, o + 86:o + 90] = inp['ret_decay_bwd'][l][None, :]
        sm[:, o + 90:o + 96] = inp['win_sink'][l][None, :]
    return sm


_NC = [None]


def make_in_maps(inp):
    sm = _smalls(inp)
    tS = _tables(True); tP = _tables(False)
    in_maps = []
    for core in range(8):
        samp = core < 4
        tabs, cbf, rt, rope = tS if samp else tP
        if samp:
            x = inp['x_sample'][core]
            cond = inp['c'][core]
            b = core
            ck = np.concatenate([inp['cache_glob_k'][b], inp['cache_win_k'][b]], axis=-1)
            ckT = np.ascontiguousarray(ck.transpose(0, 2, 3, 1))
            cv = np.concatenate([inp['cache_glob_v'][b].reshape(L, 256, 128), inp['cache_win_v'][b].reshape(L, 256, 128)], axis=-1)
            st0 = np.stack([inp['state_ret_fwd'][b], inp['state_ret_bwd'][b]], axis=1)
            st0 = np.ascontiguousarray(st0.transpose(0, 1, 3, 2, 4)).reshape(L, 2, 64, 256)
        else:
            x = inp['x_prompt'][(core - 4) * 8:(core - 3) * 8].reshape(NT, 1024)
            cond = inp['c_ctx']
            ckT = np.zeros((L, 2, 128, 256), np.float32)
            cv = np.zeros((L, 256, 256), np.float32)
            st0 = np.zeros((L, 2, 64, 256), np.float32)
        xT = np.ascontiguousarray(x.T.reshape(8, 128, NT))
        in_maps.append({
            "xT": xT, "w_in": inp['w_in'], "w_out": inp['w_out'], "w_gate_up": inp['w_gate_up'], "w_down": inp['w_down'],
            "w_mod": inp['w_mod'], "smalls": sm, "condT": np.ascontiguousarray(cond.reshape(8, 128).T), "rope": rope,
            "tabs": tabs, "cbf": cbf, "rt": rt, "ckT": np.ascontiguousarray(ckT, dtype=np.float32),
            "cv": np.ascontiguousarray(cv, dtype=np.float32), "st0": np.ascontiguousarray(st0, dtype=np.float32),
        })
    return in_maps


def kernel(**inp):
    inp = {k: np.asarray(v) for k, v in inp.items()}
    if _NC[0] is None:
        _NC[0] = build()
    nc = _NC[0]
    in_maps = make_in_maps(inp)
    res = run_bass_kernel_spmd(nc, in_maps, core_ids=list(range(8)))
    R = res.results
    y_p = np.zeros((32, 256, 1024), np.float32)
    y_s = np.zeros((4, NT, 1024), np.float32)
    ngk = np.zeros((32, L, 256, 2, 64), np.float32); ngv = np.zeros_like(ngk)
    nwk = np.zeros_like(ngk); nwv = np.zeros_like(ngk)
    nrf = np.zeros((32, L, 4, 64, 64), np.float32); nrb = np.zeros_like(nrf)
    for core in range(8):
        r = R[core]
        y = np.asarray(r["yT"]).reshape(1024, NT).T
        if core < 4:
            y_s[core] = y
            continue
        b0 = (core - 4) * 8
        y_p[b0:b0 + 8] = y.reshape(8, 256, 1024)
        okT = np.asarray(r["okT"])
        ov = np.asarray(r["ov"])
        ost = np.asarray(r["ost"])
        for s in range(8):
            tsl = slice(s * 256, (s + 1) * 256)
            ngk[b0 + s] = okT[:, :, 0:64, tsl].transpose(0, 3, 1, 2)
            nwk[b0 + s] = okT[:, :, 64:128, tsl].transpose(0, 3, 1, 2)
            ngv[b0 + s] = ov[:, tsl, 0:128].reshape(L, 256, 2, 64)
            nwv[b0 + s] = ov[:, tsl, 128:256].reshape(L, 256, 2, 64)
            nrf[b0 + s] = ost[:, 2 * s + 1, 0:64, :].reshape(L, 64, 4, 64).transpose(0, 2, 1, 3)
            nrb[b0 + s] = ost[:, 2 * s, 64:128, :].reshape(L, 64, 4, 64).transpose(0, 2, 1, 3)
    return (y_p, y_s, ngk, ngv, nrf, nrb, nwk, nwv)
```
